# Optimizing a Trainium2 kernel written in Bass

```python
import jax, jax.numpy as jnp
from jax import lax
import numpy as np

D_MODEL = 2048
BATCH = 4
SEQ = 4096
DEPTH = 1

CHUNK = 64
Q_BLOCK = 128
N_MEM = 256
EPS = 1e-6
SB_WIDTH = D_MODEL // 2
SB_HEAD_DIM = 128
SB_HEADS = SB_WIDTH // SB_HEAD_DIM
MLA_WIDTH = D_MODEL - SB_WIDTH
MLA_NOPE = 128
MLA_ROPE = 64
MLA_V = 128
MLA_HEADS = MLA_WIDTH // MLA_V
Q_LORA = D_MODEL // 4
KV_LORA = D_MODEL // 8
ROPE_THETA = 10000.0
MEM_HEADS = 4
MEM_HEAD_DIM = D_MODEL // MEM_HEADS
D_FF = 4 * D_MODEL
IN_COLS = 3 * SB_WIDTH + Q_LORA + KV_LORA + MLA_ROPE
SPLITS = (SB_WIDTH, 2 * SB_WIDTH, 3 * SB_WIDTH, 3 * SB_WIDTH + Q_LORA, 3 * SB_WIDTH + Q_LORA + KV_LORA)

kernel_name = 'hybrid_stickbreaking_mla_block'


def rmsnorm(x, g):
    xf = x.astype(jnp.float32)
    y = xf * lax.rsqrt(jnp.mean(xf * xf, axis=-1, keepdims=True) + EPS)
    return (y * g.astype(jnp.float32)).astype(x.dtype)


def rope_tables(positions):
    half = MLA_ROPE // 2
    inv_freq = ROPE_THETA ** (-jnp.arange(half, dtype=jnp.float32) / half)
    ang = positions.astype(jnp.float32)[..., None] * inv_freq
    return jnp.cos(ang), jnp.sin(ang)


def apply_rope(x, cos, sin):
    half = MLA_ROPE // 2
    xf = x.astype(jnp.float32)
    x1, x2 = xf[..., :half], xf[..., half:]
    return jnp.concatenate([x1 * cos - x2 * sin, x2 * cos + x1 * sin], axis=-1).astype(x.dtype)


def stick_breaking_attention(q, k, v):
    S = q.shape[1]
    scale = q.shape[-1] ** -0.5
    outs = []
    for start in range(0, S, Q_BLOCK):
        end = start + Q_BLOCK
        z = jnp.einsum('bqhd,bkhd->bhqk', q[:, start:end], k[:, :end]).astype(jnp.float32) * scale
        t_idx = start + jnp.arange(Q_BLOCK)[:, None]
        s_idx = jnp.arange(end)[None, :]
        before = s_idx < t_idx
        log_keep = jnp.where(before, jax.nn.log_sigmoid(-z), 0.0)
        between = lax.cumsum(log_keep, axis=3, reverse=True) - log_keep
        w = jnp.where(before, jnp.exp(jax.nn.log_sigmoid(z) + between), 0.0)
        outs.append(jnp.einsum('bhqk,bkhd->bqhd', w.astype(v.dtype), v[:, :end]))
    return jnp.concatenate(outs, axis=1)


def chunk_causal_attention(q, k, v, scale):
    S = q.shape[1]
    outs = []
    for start in range(0, S, Q_BLOCK):
        end = start + Q_BLOCK
        z = jnp.einsum('bqhd,bkhd->bhqk', q[:, start:end], k[:, :end]).astype(jnp.float32) * scale
        t_chunk = (start + jnp.arange(Q_BLOCK))[:, None] // CHUNK
        s_chunk = jnp.arange(end)[None, :] // CHUNK
        z = jnp.where(s_chunk <= t_chunk, z, -1e30)
        p = jax.nn.softmax(z, axis=-1)
        outs.append(jnp.einsum('bhqk,bkhd->bqhd', p.astype(v.dtype), v[:, :end]))
    return jnp.concatenate(outs, axis=1)


def setup_inputs(seed: int = 0) -> dict:
    key = jax.random.key(seed)
    ks = jax.random.split(key, 24)

    def w(k, shape, fan_in):
        return jax.random.normal(k, shape, jnp.float32) * (fan_in ** -0.5)

    def gain(k, shape):
        return 1.0 + 0.01 * jax.random.normal(k, shape, jnp.float32)

    L = DEPTH
    offset = jax.random.randint(ks[2], (BATCH, 1), 0, 100000, dtype=jnp.int32)
    positions = offset + jnp.arange(SEQ, dtype=jnp.int32)[None, :]
    return {
        'x': jax.random.normal(ks[0], (BATCH, SEQ, D_MODEL), jnp.float32),
        'mem': jax.random.normal(ks[1], (BATCH, N_MEM, D_MODEL), jnp.float32),
        'positions': positions,
        'norm_mix_g': gain(ks[3], (L, D_MODEL)),
        'w_in': w(ks[4], (L, D_MODEL, IN_COLS), D_MODEL),
        'q_norm_g': gain(ks[5], (L, Q_LORA)),
        'w_uq': w(ks[6], (L, Q_LORA, MLA_HEADS * (MLA_NOPE + MLA_ROPE)), Q_LORA),
        'kv_norm_g': gain(ks[7], (L, KV_LORA)),
        'w_ukv': w(ks[8], (L, KV_LORA, MLA_HEADS * (MLA_NOPE + MLA_V)), KV_LORA),
        'gn_sb_g': gain(ks[9], (L, SB_WIDTH)),
        'gn_mla_g': gain(ks[10], (L, MLA_WIDTH)),
        'w_out': w(ks[11], (L, D_MODEL, D_MODEL), D_MODEL),
        'norm_mem_g': gain(ks[12], (L, D_MODEL)),
        'mem_kv_norm_g': gain(ks[13], (L, D_MODEL)),
        'w_mq': w(ks[14], (L, D_MODEL, D_MODEL), D_MODEL),
        'w_mkv': w(ks[15], (L, D_MODEL, 2 * D_MODEL), D_MODEL),
        'w_mo': w(ks[16], (L, D_MODEL, D_MODEL), D_MODEL),
        'norm_ffn_g': gain(ks[17], (L, D_MODEL)),
        'w_ff1': w(ks[18], (L, D_MODEL, D_FF), D_MODEL),
        'w_ff2': w(ks[19], (L, D_FF, D_MODEL), D_FF),
        'final_norm_g': gain(ks[20], (D_MODEL,)),
    }


def reference(x, mem, positions, norm_mix_g, w_in, q_norm_g, w_uq, kv_norm_g, w_ukv, gn_sb_g, gn_mla_g, w_out, norm_mem_g, mem_kv_norm_g, w_mq, w_mkv, w_mo, norm_ffn_g, w_ff1, w_ff2, final_norm_g):
    B, S, D = x.shape
    M = mem.shape[1]
    cos, sin = rope_tables(positions)
    for l in range(DEPTH):
        h = rmsnorm(x, norm_mix_g[l])
        proj = h @ w_in[l]
        q_sb, k_sb, v_sb, c_q, c_kv, k_pe = jnp.split(proj, SPLITS, axis=-1)

        o_sb = stick_breaking_attention(
            q_sb.reshape(B, S, SB_HEADS, SB_HEAD_DIM),
            k_sb.reshape(B, S, SB_HEADS, SB_HEAD_DIM),
            v_sb.reshape(B, S, SB_HEADS, SB_HEAD_DIM)).reshape(B, S, SB_WIDTH)

        q = (rmsnorm(c_q, q_norm_g[l]) @ w_uq[l]).reshape(B, S, MLA_HEADS, MLA_NOPE + MLA_ROPE)
        q_nope, q_pe = q[..., :MLA_NOPE], q[..., MLA_NOPE:]
        q_pe = apply_rope(q_pe, cos[:, :, None, :], sin[:, :, None, :])
        k_pe = apply_rope(k_pe, cos, sin)
        kv = (rmsnorm(c_kv, kv_norm_g[l]) @ w_ukv[l]).reshape(B, S, MLA_HEADS, MLA_NOPE + MLA_V)
        k_nope, v_mla = kv[..., :MLA_NOPE], kv[..., MLA_NOPE:]
        q_mla = jnp.concatenate([q_nope, q_pe], axis=-1)
        k_mla = jnp.concatenate([k_nope, jnp.broadcast_to(k_pe[:, :, None, :], (B, S, MLA_HEADS, MLA_ROPE))], axis=-1)
        o_mla = chunk_causal_attention(q_mla, k_mla, v_mla, (MLA_NOPE + MLA_ROPE) ** -0.5).reshape(B, S, MLA_WIDTH)

        o = jnp.concatenate([rmsnorm(o_sb, gn_sb_g[l]), rmsnorm(o_mla, gn_mla_g[l])], axis=-1)
        x = x + o @ w_out[l]

        h = rmsnorm(x, norm_mem_g[l])
        m = rmsnorm(mem, mem_kv_norm_g[l])
        qm = (h @ w_mq[l]).reshape(B, S, MEM_HEADS, MEM_HEAD_DIM)
        km, vm = jnp.split(m @ w_mkv[l], 2, axis=-1)
        km = km.reshape(B, M, MEM_HEADS, MEM_HEAD_DIM)
        vm = vm.reshape(B, M, MEM_HEADS, MEM_HEAD_DIM)
        zm = jnp.einsum('bqhd,bmhd->bhqm', qm, km).astype(jnp.float32) * (MEM_HEAD_DIM ** -0.5)
        pm = jax.nn.softmax(zm, axis=-1)
        om = jnp.einsum('bhqm,bmhd->bqhd', pm.astype(vm.dtype), vm).reshape(B, S, D)
        x = x + om @ w_mo[l]

        h = rmsnorm(x, norm_ffn_g[l])
        x = x + jnp.square(jax.nn.relu(h @ w_ff1[l])) @ w_ff2[l]
    return rmsnorm(x, final_norm_g)
```

```python
import numpy as np
import concourse.bass as bass
import concourse.mybir as mybir
from concourse.bass_utils import run_bass_kernel_spmd

F32 = mybir.dt.float32
BF16 = mybir.dt.bfloat16
I32 = mybir.dt.int32
AF = mybir.ActivationFunctionType
ALU = mybir.AluOpType
AX = mybir.AxisListType

D = 2048
S = 4096
NOWN = 2048
NMEM = 256
EPS = 1e-6
INCOLS = 3904
DFF = 8192
TWO_PI = 6.283185307179586
C1 = 6.28125
C2 = TWO_PI - C1
PI = 3.141592653589793
SEM_LIM = 30000
NDMA_SEMS = 14


def kb_count(slot):
    g, par = divmod(slot, 2)
    return 4 * g + 2 if par == 0 else 4 * g + 4


class Buf:
    __slots__ = ("name", "w", "r")

    def __init__(self, name):
        self.name = name
        self.w = None
        self.r = {}


class T:
    __slots__ = ("ap", "bufs")

    def __init__(self, ap, bufs):
        self.ap = ap
        self.bufs = list(bufs)

    def __getitem__(self, idx):
        return T(self.ap[idx], self.bufs)

    def re(self, pat, **kw):
        return T(self.ap.rearrange(pat, **kw), self.bufs)


class Op:
    __slots__ = ("eng", "fn", "deps", "sig", "cnt", "dma", "sem", "val")

    def __init__(self, eng, fn, dma):
        self.eng = eng
        self.fn = fn
        self.dma = dma
        self.deps = []
        self.sig = False
        self.cnt = None
        self.sem = None
        self.val = None


ENGS = ["pe", "act", "dve", "pool", "sp"]


class Prog:
    def __init__(self):
        self.ops = {e: [] for e in ENGS}
        self.ndma = {e: 0 for e in ENGS}
        self.dma_ops = {e: [] for e in ENGS}
        self.all_dma = []
        self.uid = 0

    def add(self, eng, fn, reads=(), writes=(), dma=False, extra=()):
        op = Op(eng, fn, dma)
        deps = []
        seen = set()

        def push(d, raw):
            if d is None or id(d) in seen:
                return
            if (not d.dma) and d.eng == eng and not dma:
                if eng == "pe":
                    return
            if (not d.dma) and d.eng == eng and dma and eng != "pool":
                return
            seen.add(id(d))
            deps.append(d)

        for b in reads:
            push(b.w, True)
        for b in writes:
            push(b.w, False)
            for o in b.r.values():
                push(o, False)
        for d in extra:
            push(d, True)
        if dma:
            k = self.ndma[eng]
            self.ndma[eng] += 1
            op.sem = (eng, k % NDMA_SEMS)
            op.val = 16 * (k // NDMA_SEMS + 1)
            if k >= NDMA_SEMS:
                push(self.dma_ops[eng][k - NDMA_SEMS], False)
            self.dma_ops[eng].append(op)
            self.all_dma.append(op)
        for d in deps:
            d.sig = True
        op.deps = deps
        self.ops[eng].append(op)
        for b in reads:
            if dma:
                self.uid += 1
                b.r[("dma", self.uid)] = op
            else:
                b.r[eng] = op
        for b in writes:
            b.w = op
            b.r = {}
        return op

    def barrier(self):
        lasts = []
        for e in ENGS:
            for op in reversed(self.ops[e]):
                if op.fn is not None:
                    lasts.append(op)
                    break
        dm = list(self.all_dma)
        self.all_dma = []
        for e in ENGS:
            self.add(e, None, extra=lasts + dm)

    def emit(self, nc, stack):
        nsem = {}
        for e in ENGS:
            c = 0
            for op in self.ops[e]:
                if op.sig and not op.dma:
                    op.sem = (e + "_c", c // SEM_LIM)
                    op.val = c % SEM_LIM + 1
                    c += 1
        sems = {}

        def getsem(key):
            if key not in sems:
                sems[key] = stack.enter_context(nc.semaphore("s_%s_%d" % key))
            return sems[key]

        for e in ENGS:
            for op in self.ops[e]:
                if op.sem is not None:
                    getsem(op.sem)
        block = stack.enter_context(nc.Block())

        def run(eng_name):
            def body(e):
                waited = {}
                for op in self.ops[eng_name]:
                    for d in op.deps:
                        if waited.get(d.sem, 0) >= d.val:
                            continue
                        waited[d.sem] = d.val
                        e.wait_ge(sems[d.sem], d.val)
                    if op.fn is None:
                        continue
                    ins = op.fn(e)
                    if op.dma:
                        ins.then_inc(sems[op.sem], 16)
                    elif op.sig:
                        ins.then_inc(sems[op.sem], 1)
            return body

        block.tensor(run("pe"))
        block.scalar(run("act"))
        block.vector(run("dve"))
        block.gpsimd(run("pool"))
        block.sync(run("sp"))


class KB:
    def __init__(self, nc, stack):
        self.nc = nc
        self.stack = stack
        self.p = Prog()
        self.nbuf = 0

    def buf(self, name="b"):
        self.nbuf += 1
        return Buf("%s%d" % (name, self.nbuf))

    def sb(self, name, shape, dt):
        t = self.stack.enter_context(self.nc.sbuf_tensor("sb_" + name, list(shape), dt))
        idx = tuple(slice(None) for _ in shape)
        return T(t[idx], [self.buf(name)])

    def dram(self, name, shape, dt, kind):
        t = self.nc.dram_tensor(name, list(shape), dt, kind=kind)
        return T(t.ap(), [])

    def mm(self, out, lhsT, rhs, start, stop):
        o, l, r = out.ap, lhsT.ap, rhs.ap
        self.p.add("pe", lambda e: e.matmul(o, l, r, start=start, stop=stop, skip_group_check=True),
                   reads=lhsT.bufs + rhs.bufs, writes=out.bufs)

    def tr(self, out, in_, ident):
        o, i, d = out.ap, in_.ap, ident.ap
        self.p.add("pe", lambda e: e.transpose(o, i, d), reads=in_.bufs + ident.bufs, writes=out.bufs)

    def act(self, out, in_, func, bias=0.0, scale=1.0, accum=None):
        o, i = out.ap, in_.ap
        rd = list(in_.bufs)
        wr = list(out.bufs)
        b = bias
        s = scale
        if isinstance(bias, T):
            rd += bias.bufs
            b = bias.ap
        if isinstance(scale, T):
            rd += scale.bufs
            s = scale.ap
        a = None
        if accum is not None:
            wr += accum.bufs
            a = accum.ap
        if a is None:
            fn = lambda e: e.activation(o, i, func, bias=b, scale=s)
        else:
            fn = lambda e: e.activation(o, i, func, bias=b, scale=s, accum_out=a)
        self.p.add("act", fn, reads=rd, writes=wr)

    def tt(self, eng, out, a, b, op):
        o, x, y = out.ap, a.ap, b.ap
        self.p.add(eng, lambda e: e.tensor_tensor(o, x, y, op), reads=a.bufs + b.bufs, writes=out.bufs)

    def ts(self, eng, out, a, s1, op0, s2=None, op1=None):
        o, x = out.ap, a.ap
        rd = list(a.bufs)
        v1, v2 = s1, s2
        if isinstance(s1, T):
            rd += s1.bufs
            v1 = s1.ap
        if isinstance(s2, T):
            rd += s2.bufs
            v2 = s2.ap
        if op1 is None:
            fn = lambda e: e.tensor_scalar(o, x, v1, None, op0)
        else:
            fn = lambda e: e.tensor_scalar(o, x, v1, v2, op0, op1)
        self.p.add(eng, fn, reads=rd, writes=out.bufs)

    def stt(self, out, in0, scalar, in1, op0, op1):
        o, x, y = out.ap, in0.ap, in1.ap
        rd = in0.bufs + in1.bufs
        v = scalar
        if isinstance(scalar, T):
            rd = rd + scalar.bufs
            v = scalar.ap
        self.p.add("dve", lambda e: e.scalar_tensor_tensor(o, x, v, y, op0, op1), reads=rd, writes=out.bufs)

    def copy(self, eng, out, in_):
        o, i = out.ap, in_.ap
        if eng == "act":
            self.p.add("act", lambda e: e.copy(o, i), reads=in_.bufs, writes=out.bufs)
        else:
            self.p.add(eng, lambda e: e.tensor_copy(o, i), reads=in_.bufs, writes=out.bufs)

    def memset(self, eng, out, val):
        o = out.ap
        self.p.add(eng, lambda e: e.memset(o, val), writes=out.bufs)

    def recip(self, out, in_):
        o, i = out.ap, in_.ap
        self.p.add("dve", lambda e: e.reciprocal(o, i), reads=in_.bufs, writes=out.bufs)

    def dma(self, q, out, in_):
        o, i = out.ap, in_.ap
        return self.p.add(q, lambda e: e.dma_start(o, i), reads=in_.bufs, writes=out.bufs, dma=True)


def build_program(stage=99, debug=False):
    from contextlib import ExitStack
    nc = bass.Bass("TRN2", target_bir_lowering=False)
    stack = ExitStack()
    k = KB(nc, stack)
    p = k.p

    def din(name, shape, dt=F32):
        return k.dram(name, shape, dt, "ExternalInput")

    xall = din("xall", [S, D])
    xown = din("xown", [NOWN, D])
    memx = din("memx", [NMEM, D])
    posall = din("posall", [1, S], I32)
    posown = din("posown", [1, NOWN], I32)
    w_in = din("w_in", [D, INCOLS])
    w_kpesw = din("w_kpesw", [D, 64])
    w_uqn = din("w_uqn", [512, 1024])
    w_uqp = din("w_uqp", [512, 512])
    w_uqps = din("w_uqps", [512, 512])
    w_ukvk = din("w_ukvk", [256, 1024])
    w_ukvv = din("w_ukvv", [256, 1024])
    w_out = din("w_out", [D, D])
    w_mq = din("w_mq", [D, D])
    w_mkv = din("w_mkv", [D, 2 * D])
    w_mo = din("w_mo", [D, D])
    w_ff1 = din("w_ff1", [D, DFF])
    w_ff2 = din("w_ff2", [DFF, D])
    gpack_d = din("gpack", [128, 96])
    gfin_d = din("gfin", [128, D])
    cpack_d = din("cpack", [128, 13 * 128])
    rpack_d = din("rpack", [64, 2])
    out_d = k.dram("out", [NOWN, D], F32, "ExternalOutput")

    def dsc(name, shape, dt=BF16):
        return k.dram(name, shape, dt, "ExternalOutput" if debug else "Internal")

    def finish():
        p.barrier()
        p.emit(nc, stack)
        stack.close()
        return nc

    kT_d = dsc("kT_d", [128, 8, S])
    v_d = dsc("v_d", [S, 1024])
    hT_d = dsc("hT_d", [128, 16, S])
    ckvT_d = dsc("ckvT_d", [128, 2, S])
    kpeT_d = dsc("kpeT_d", [64, S])
    qT_d = dsc("qT_d", [128, 8, NOWN])
    cqT_d = dsc("cqT_d", [128, 4, NOWN])
    ropeC_d = dsc("ropeC_d", [64, NOWN], F32)
    ropeS_d = dsc("ropeS_d", [64, NOWN], F32)

    gpack = k.sb("gpack", [128, 96], F32)
    ident = k.sb("ident", [128, 128], F32)
    negU = k.sb("negU", [128, 128], BF16)
    negones = k.sb("negones", [128, 128], BF16)
    ones = k.sb("ones", [128, 128], BF16)
    msb = k.sb("msb", [128, 512], BF16)
    mmla = k.sb("mmla", [128, 512], BF16)
    rpack = k.sb("rpack", [64, 2], F32)
    kmT = k.sb("kmT", [128, 16, NMEM], BF16)
    vm = k.sb("vm", [128, 2, D], BF16)
    stat = k.sb("stat", [128, 32], F32)
    ones32 = k.sb("ones32", [128, 128], F32)

    NAR = 47960
    AR = stack.enter_context(nc.sbuf_tensor("AR", [128, NAR], F32))

    class Arena:
        def __init__(self):
            self.off = 0

        def reset(self, off=0):
            self.off = off

        def raw(self, n32, parts, dt, cols):
            assert self.off + n32 <= NAR, ("arena overflow", self.off, n32, NAR)
            ap = AR[0:parts, self.off:self.off + n32]
            self.off += n32
            if dt != F32:
                ap = ap.bitcast(dt)
                if dt == BF16:
                    ap = ap[:, 0:cols]
            return ap

        def alloc(self, name, cols, dt=BF16, parts=128, shape=None):
            n32 = cols if dt != BF16 else (cols + 1) // 2
            ap = self.raw(n32, parts, dt, cols)
            t = T(ap, [k.buf(name)])
            if shape is not None:
                names = " ".join("d%d" % i for i in range(len(shape)))
                kw = {"d%d" % i: shape[i] for i in range(len(shape))}
                t = t.re("p (%s) -> p %s" % (names, names), **kw)
            return t

        def multi(self, name, n, cols, dt=BF16, parts=128):
            n32 = cols if dt != BF16 else cols // 2
            off0 = self.off
            ts = [self.alloc("%s%d" % (name, i), cols, dt, parts) for i in range(n)]
            ap = AR[0:parts, off0:off0 + n * n32]
            if dt != F32:
                ap = ap.bitcast(dt)
            comb = T(ap.rearrange("p (a b) -> p a b", a=n), [b for t in ts for b in t.bufs])
            return ts, comb

    ar = Arena()

    banks = []
    for i in range(8):
        t = stack.enter_context(nc.psum_tensor("ps%d" % i, [128, 512], F32))
        banks.append(T(t[:, :], [k.buf("ps")]))
    rot = {"i": 0}

    def ps(n=6):
        b = banks[rot["i"] % n]
        rot["i"] += 1
        return b

    gcol = {"mix": 0, "mem": 16, "ffn": 32, "memkv": 48, "q": 64, "kv": 68, "sb": 72, "mla": 80}

    def g_of(name, kc):
        c = gcol[name] + kc
        return gpack[:, c:c + 1]

    k.dma("sp", gpack, gpack_d)
    k.dma("sp", rpack, rpack_d)
    k.dma("sp", ident, cpack_d[:, 0:128])
    cst = ar.alloc("cst", 12 * 128, F32)
    k.dma("sp", cst, cpack_d[:, 128:128 + 12 * 128])
    k.copy("dve", negU, cst[:, 0:128])
    k.copy("dve", negones, cst[:, 128:256])
    k.copy("dve", ones, cst[:, 256:384])
    k.copy("dve", ones32, cst[:, 256:384])
    k.copy("dve", msb, cst[:, 512:1024])
    k.copy("dve", mmla, cst[:, 1024:1536])
    sstat = {"i": 0}

    def st1():
        c = sstat["i"] % 32
        sstat["i"] += 1
        return stat[:, c:c + 1]

    evac_flip = {"i": 0}

    def evac(out, in_, scale=None, eng=None):
        if eng is None:
            evac_flip["i"] += 1
            eng = "act" if evac_flip["i"] % 2 == 0 else "dve"
        if eng == "act":
            if scale is None:
                k.act(out, in_, AF.Copy)
            else:
                k.act(out, in_, AF.Copy, scale=scale)
        else:
            if scale is None:
                k.copy("dve", out, in_)
            else:
                k.ts("dve", out, in_, scale, ALU.mult)

    def rstd_of(ss, dim):
        lnv = st1()
        r = st1()
        k.act(lnv, ss, AF.Ln, bias=EPS, scale=1.0 / dim)
        k.act(r, lnv, AF.Exp, scale=-0.5)
        return r

    def norm_rows(xt, dim, junk):
        ss = st1()
        k.act(junk, xt, AF.Square, accum=ss)
        r = rstd_of(ss, dim)
        k.ts("dve", xt, xt, r, ALU.mult)

    def transpose_to(xts, nkc, gname, dst_fn):
        nb = len(xts)
        for kc in range(nkc):
            bank = ps()
            for t, xt in enumerate(xts):
                k.tr(bank[:, t * 128:(t + 1) * 128], xt[:, kc * 128:(kc + 1) * 128], ident)
            evac(dst_fn(kc), bank[:, 0:nb * 128], scale=g_of(gname, kc), eng=("act" if kc % 2 else "dve"))

    def load_panel(dst, w, r0, nk, c0, ncols):
        src = w[r0:r0 + nk * 128, c0:c0 + ncols].re("(kc p) c -> p kc c", p=128)
        return k.dma("pool", dst, src)

    def rope_tables(pos_d, t0, n, Cd, Sd, tmp):
        pi_, a, kk, r = tmp
        k.dma("sp", pi_, T(pos_d.ap[0:1, t0:t0 + n].partition_broadcast(64)[:, 0, :], []))
        k.copy("dve", a, pi_)
        k.ts("dve", a, a, rpack[:, 0:1], ALU.mult)
        k.ts("dve", kk, a, 1.0 / TWO_PI, ALU.mult)
        k.ts("dve", kk, kk, 12582912.0, ALU.add)
        k.ts("dve", kk, kk, 12582912.0, ALU.subtract)
        k.stt(r, kk, -C1, a, ALU.mult, ALU.add)
        k.stt(r, kk, -C2, r, ALU.mult, ALU.add)

        def wrap(x):
            k.ts("dve", kk, x, PI, ALU.is_gt, -TWO_PI, ALU.mult)
            k.tt("dve", x, x, kk, ALU.add)
            k.ts("dve", kk, x, -PI, ALU.is_lt, TWO_PI, ALU.mult)
            k.tt("dve", x, x, kk, ALU.add)
            k.ts("dve", x, x, PI, ALU.min, -PI, ALU.max)

        wrap(r)
        k.act(Sd, r, AF.Sin)
        k.ts("dve", Sd, Sd, rpack[:, 1:2], ALU.mult)
        k.ts("dve", r, r, PI / 2, ALU.add)
        wrap(r)
        k.act(Cd, r, AF.Sin)

    mt = [ar.alloc("memt%d" % i, D, F32) for i in range(2)]
    junk = ar.alloc("junk", D, F32)
    mTk, mT = ar.multi("mT", 16, NMEM)
    wp = [ar.alloc("wp%d" % i, 16 * 512, shape=[16, 512]) for i in range(3)]
    for i in range(2):
        k.dma("sp", mt[i], memx[i * 128:(i + 1) * 128, :])
        norm_rows(mt[i], D, junk)
    import os
    ksub = int(os.environ.get("KSUB", "99"))
    if ksub == 0:
        return finish()
    if ksub >= 2:
        transpose_to(mt, 16, "memkv", lambda kc: mTk[kc])
    if ksub <= 2:
        return finish()
    for pn in range(8 if ksub > 3 else 1):
        w = wp[pn % 3]
        load_panel(w, w_mkv, 0, 16, pn * 512, 512)
        if pn < 4:
            for c in range(4):
                bank = ps()
                for kc in range(16):
                    k.mm(bank[:, 0:NMEM], w[:, kc, c * 128:(c + 1) * 128], mTk[kc], kc == 0, kc == 15)
                evac(kmT[:, pn * 4 + c, :], bank[:, 0:NMEM])
        else:
            for mb in range(2):
                bank = ps()
                for kc in range(16):
                    k.mm(bank, mTk[kc][:, mb * 128:(mb + 1) * 128], w[:, kc, :], kc == 0, kc == 15)
                evac(vm[:, mb, (pn - 4) * 512:(pn - 3) * 512], bank)
    p.barrier()
    if debug:
        dbg_km = k.dram("dbg_km", [128, 16, NMEM], BF16, "ExternalOutput")
        dbg_vm = k.dram("dbg_vm", [128, 2, D], BF16, "ExternalOutput")
        k.dma("sp", dbg_km, kmT)
        k.dma("sp", dbg_vm, vm)
    if stage == 0:
        return finish()

    def load_x4(xsrc, t0, xt4):
        k.dma("pool", xt4, xsrc[t0:t0 + 512, :].re("(t p) d -> p t d", p=128))

    def nt_norm(xt4, junk):
        for t in range(4):
            norm_rows(xt4[:, t, :], D, junk)

    def nt_tr(xt4, hk):
        transpose_to([xt4[:, t, :] for t in range(4)], 16, "mix", lambda kc: hk[kc])

    def latent(h_k, WA, c0, nfeat, gname, cst4, junk, stg_k):
        for t in range(4):
            bank = ps()
            for kc in range(16):
                k.mm(bank[:, 0:nfeat], h_k[kc][:, t * 128:(t + 1) * 128], WA[:, kc, c0:c0 + nfeat], kc == 0, kc == 15)
            ss = st1()
            k.act(junk[:, 0:nfeat], bank[:, 0:nfeat], AF.Square, accum=ss)
            r = rstd_of(ss, nfeat)
            k.ts("dve", cst4[:, t, 0:nfeat], bank[:, 0:nfeat], r, ALU.mult)
        transpose_to([cst4[:, t, 0:nfeat] for t in range(4)], nfeat // 128, gname, lambda kc: stg_k[kc])

    ar.reset()
    NA = 1472
    WA = ar.alloc("WA", 16 * NA, shape=[16, NA])
    hks = [ar.multi("hT%d_" % i, 16, 512) for i in range(2)]
    kst_k, kst = ar.multi("kst", 8, 512)
    ck_k, ckst = ar.multi("ckst", 2, 512)
    kpst = ar.alloc("kpst", 512, parts=64)
    xt4s = [ar.alloc("xt4_%d" % i, 4 * D, F32, shape=[4, D]) for i in range(2)]
    junk = ar.alloc("junk", D, F32)
    cst4 = ar.alloc("cst4", 4 * 256, F32, shape=[4, 256])
    junk2 = ar.alloc("junk2", 256, F32)
    rt = [ar.alloc("rt%d" % i, 512, F32, parts=64) for i in range(7)]
    rta = [ar.alloc("rta%d" % i, 512, F32, parts=64) for i in range(2)]
    posi = ar.alloc("posi", 512, I32, parts=64)
    load_x4(xall, 0, xt4s[0])
    load_x4(xall, 512, xt4s[1])
    load_panel(WA[:, :, 0:1024], w_in, 0, 16, 1024, 1024)
    load_panel(WA[:, :, 1024:1344], w_in, 0, 16, 3584, 320)
    load_panel(WA[:, :, 1344:1408], w_kpesw, 0, 16, 0, 64)
    def p1a_tr(ch):
        t0 = ch * 512
        h_k, h_all = hks[ch % 2]
        nt_tr(xt4s[ch % 2], h_k)
        k.dma("sp", hT_d[:, :, t0:t0 + 512], h_all)

    rtab = [(rt[0], rt[1]), (rta[0], rta[1])]

    def p1a_proj(ch):
        t0 = ch * 512
        h_k, h_all = hks[ch % 2]
        rC, rS = rtab[ch % 2]
        if ch == 0:
            rope_tables(posall, t0, 512, rC, rS, (posi, rt[2], rt[3], rt[4]))
        for hd in range(8):
            bank = ps()
            for kc in range(16):
                k.mm(bank, WA[:, kc, hd * 128:(hd + 1) * 128], h_k[kc], kc == 0, kc == 15)
            evac(kst_k[hd], bank)
        k.dma("sp", kT_d[:, :, t0:t0 + 512], kst)
        latent(h_k, WA, 1024, 256, "kv", cst4, junk2, ck_k)
        k.dma("sp", ckvT_d[:, :, t0:t0 + 512], ckst)
        if ch + 2 < 8:
            nt_norm(xt4s[ch % 2], junk)
        b1 = ps()
        b2 = ps()
        for kc in range(16):
            k.mm(b1[0:64, :], WA[:, kc, 1280:1344], h_k[kc], kc == 0, kc == 15)
        for kc in range(16):
            k.mm(b2[0:64, :], WA[:, kc, 1344:1408], h_k[kc], kc == 0, kc == 15)
        k.tt("dve", rt[5], b1[0:64, :], rC, ALU.mult)
        k.tt("dve", rt[6], b2[0:64, :], rS, ALU.mult)
        k.tt("dve", kpst, rt[5], rt[6], ALU.add)
        k.dma("sp", kpeT_d[:, t0:t0 + 512], kpst)
        if ch + 1 < 8:
            nC, nS = rtab[(ch + 1) % 2]
            rope_tables(posall, t0 + 512, 512, nC, nS, (posi, rt[2], rt[3], rt[4]))

    nt_norm(xt4s[0], junk)
    p1a_tr(0)
    nt_norm(xt4s[1], junk)
    for ch in range(8):
        if ch + 1 < 8:
            p1a_tr(ch + 1)
        if ch + 2 < 8:
            load_x4(xall, (ch + 2) * 512, xt4s[ch % 2])
        p1a_proj(ch)
    p.barrier()
    if stage == 1:
        return finish()
    ar.reset()
    WA = ar.alloc("WV", 16 * 1024, shape=[16, 1024])
    hks = [ar.multi("hV%d_" % i, 16, 512) for i in range(2)]
    vsts = [ar.multi("vst%d_" % i, 4, 1024) for i in range(2)]
    load_panel(WA, w_in, 0, 16, 2048, 1024)
    k.dma("pool", hks[0][1], hT_d[:, :, 0:512])
    for ch in range(8):
        t0 = ch * 512
        h_k, h_all = hks[ch % 2]
        vs_k, vs_all = vsts[ch % 2]
        if ch + 1 < 8:
            k.dma("pool", hks[(ch + 1) % 2][1], hT_d[:, :, t0 + 512:t0 + 1024])
        for t in range(4):
            for hf in range(2):
                bank = ps()
                for kc in range(16):
                    k.mm(bank, h_k[kc][:, t * 128:(t + 1) * 128], WA[:, kc, hf * 512:(hf + 1) * 512], kc == 0, kc == 15)
                evac(vs_k[t][:, hf * 512:(hf + 1) * 512], bank, eng=("act" if t % 2 else "dve"))
        k.dma("sp", v_d[t0:t0 + 512, :].re("(t p) c -> p t c", p=128), vs_all)
    p.barrier()
    if stage == 2:
        return finish()
    ar.reset()
    WA = ar.alloc("WB", 16 * 1536, shape=[16, 1536])
    hks = [ar.multi("hB%d_" % i, 16, 512) for i in range(2)]
    kst_k, kst = ar.multi("qst", 8, 512)
    cq_k, cqst = ar.multi("cqst", 4, 512)
    xt4s = [ar.alloc("xt4b_%d" % i, 4 * D, F32, shape=[4, D]) for i in range(2)]
    junk = ar.alloc("junkb", D, F32)
    cst4 = ar.alloc("cst4b", 4 * 512, F32, shape=[4, 512])
    junk2 = junk
    rt = [ar.alloc("rtb%d" % i, 512, F32, parts=64) for i in range(5)]
    posi = ar.alloc("posib", 512, I32, parts=64)
    load_x4(xown, 0, xt4s[0])
    load_x4(xown, 512, xt4s[1])
    load_panel(WA[:, :, 0:1024], w_in, 0, 16, 0, 1024)
    load_panel(WA[:, :, 1024:1536], w_in, 0, 16, 3072, 512)
    def p1b_tr(ch):
        h_k, h_all = hks[ch % 2]
        nt_tr(xt4s[ch % 2], h_k)

    def p1b_proj(ch):
        t0 = ch * 512
        h_k, h_all = hks[ch % 2]
        for hd in range(8):
            bank = ps()
            for kc in range(16):
                k.mm(bank, WA[:, kc, hd * 128:(hd + 1) * 128], h_k[kc], kc == 0, kc == 15)
            evac(kst_k[hd], bank, scale=128.0 ** -0.5)
        k.dma("sp", qT_d[:, :, t0:t0 + 512], kst)
        latent(h_k, WA, 1024, 512, "q", cst4, junk2, cq_k)
        k.dma("sp", cqT_d[:, :, t0:t0 + 512], cqst)
        if ch + 2 < 4:
            nt_norm(xt4s[ch % 2], junk)
        rope_tables(posown, t0, 512, rt[0], rt[1], (posi, rt[2], rt[3], rt[4]))
        k.dma("sp", ropeC_d[:, t0:t0 + 512], rt[0])
        k.dma("sp", ropeS_d[:, t0:t0 + 512], rt[1])

    nt_norm(xt4s[0], junk)
    p1b_tr(0)
    nt_norm(xt4s[1], junk)
    for ch in range(4):
        if ch + 1 < 4:
            p1b_tr(ch + 1)
        if ch + 2 < 4:
            load_x4(xown, (ch + 2) * 512, xt4s[ch % 2])
        p1b_proj(ch)
    p.barrier()
    if stage == 3:
        return finish()

    dbg_of = k.dram("dbg_of", [16, 128, 1024], F32, "ExternalOutput") if debug else None

    def dump_x(xs_, tok0_):
        for tt_ in range(8):
            k.dma("sp", out_d[tok0_ + tt_ * 128: tok0_ + (tt_ + 1) * 128, :], xs_[tt_])

    for st in range(2):
        tok0 = st * 1024
        nkb_max = kb_count(8 * st + 7)
        nk = nkb_max * 128
        ar.reset()
        of32 = [ar.alloc("of32_%d" % i, 1024, F32) for i in range(16)]
        tmpf = [ar.alloc("tmpf%d" % i, 512, F32) for i in range(3)]
        base_off = ar.off

        def group_geom(gq):
            s0 = 8 * st + 4 * gq
            n0 = kb_count(s0)
            return s0, n0, n0 + 6

        def kb_geom(n0, kb):
            if kb < n0 - 2:
                return 0, None
            j0 = (kb - n0 + 2) // 2
            which = (kb - n0) % 2
            return j0, 2 * (j0 % 2) + which

        ef = [ar.alloc("ef%d" % i, 512, F32) for i in range(3)]
        kTh = [ar.alloc("kTh%d" % i, S) for i in range(2)]
        vh = [ar.alloc("vh%d" % i, 32 * 128, shape=[32, 128]) for i in range(2)]
        qTh = [ar.alloc("qTh%d" % i, 1024) for i in range(2)]
        spb = [ar.alloc("spb%d" % i, 512) for i in range(4)]
        wtb = [ar.alloc("wtb%d" % i, 512) for i in range(4)]
        Rb = [[ar.alloc("Rb%d_%d" % (g_, i), 512) for i in range(3)] for g_ in range(2)]
        units = []
        for hd in range(8):
            for gq in range(2):
                s0, n0, nmax = group_geom(gq)
                for kb in range(nmax - 1, -1, -1):
                    units.append((hd, gq, kb, n0, nmax))
        state = {}

        def sb_geo(u):
            hd, gq, kb, n0, nmax = u
            j0, mi = kb_geom(n0, kb)
            c0 = j0 * 128
            q_ = qTh[hd % 2]
            kT_ = kTh[hd % 2]
            qc = q_[:, gq * 512 + c0: gq * 512 + 512]
            kblk = kT_[:, kb * 128:(kb + 1) * 128]
            return hd, gq, kb, n0, nmax, c0, mi, qc, kblk

        def sb_s1(u, i):
            hd, gq, kb, n0, nmax, c0, mi, qc, kblk = sb_geo(u)
            if gq == 0 and kb == nmax - 1:
                k.dma("sp", kTh[hd % 2][:, 0:nk], kT_d[:, hd, 0:nk])
                k.dma("sp", vh[hd % 2][:, 0:nkb_max, :],
                      v_d[0:nk, hd * 128:(hd + 1) * 128].re("(b p) c -> p b c", p=128))
                k.dma("sp", qTh[hd % 2], qT_d[:, hd, tok0:tok0 + 1024])
            A = ps()
            k.mm(A[:, c0:512], kblk, qc, True, True)
            e_ = ef[i % 3]
            sp_ = spb[i % 4]
            k.act(e_[:, c0:512], A[:, c0:512], AF.Exp)
            k.act(sp_[:, c0:512], e_[:, c0:512], AF.Ln, bias=1.0)
            if mi is not None:
                k.tt("pool", sp_[:, c0:c0 + 128], sp_[:, c0:c0 + 128], msb[:, mi * 128:(mi + 1) * 128], ALU.mult)
            Rc = Rb[gq % 2][i % 3]
            Rn = Rb[gq % 2][(i + 1) % 3]
            if kb == nmax - 1:
                for rb_ in Rb[gq % 2]:
                    k.memset("dve", rb_, 0.0)
            if kb > 0:
                k.tt("dve", Rn[:, c0:512], Rc[:, c0:512], sp_[:, c0:512], ALU.add)

        def sb_s2(u, i):
            hd, gq, kb, n0, nmax, c0, mi, qc, kblk = sb_geo(u)
            sp_ = spb[i % 4]
            w_ = wtb[i % 4]
            Rc = Rb[gq % 2][i % 3]
            first = kb == nmax - 1
            Bk = ps()
            k.mm(Bk[:, c0:512], kblk, qc, True, False)
            k.mm(Bk[:, c0:512], negU, sp_[:, c0:512], False, first)
            if not first:
                k.mm(Bk[:, c0:512], negones, Rc[:, c0:512], False, True)
            k.act(w_[:, c0:512], Bk[:, c0:512], AF.Exp)
            if mi is not None:
                k.tt("pool", w_[:, c0:c0 + 128], w_[:, c0:c0 + 128], msb[:, mi * 128:(mi + 1) * 128], ALU.mult)

        def sb_s3(u, i):
            hd, gq, kb, n0, nmax, c0, mi, qc, kblk = sb_geo(u)
            w_ = wtb[i % 4]
            O = banks[6 + gq % 2]
            if kb == nmax - 1:
                k.memset("dve", O, 0.0)
            k.mm(O[:, c0:512], vh[hd % 2][:, kb, :], w_[:, c0:512], False, kb == 0)
            if kb == 0:
                k.copy("act", of32[hd][:, gq * 512:(gq + 1) * 512], O)

        def skewed(units, stages, lags):
            n = len(units)
            for i in range(n + max(lags)):
                for s, lag in zip(stages, lags):
                    j = i - lag
                    if 0 <= j < n:
                        s(units[j], j)

        skewed(units, (sb_s1, sb_s2, sb_s3), (0, 1, 2))
        p.barrier()
        if stage == 4:
            for i in range(8):
                k.dma("sp", dbg_of[i], of32[i])
            return finish()

        ar.reset(base_off)
        ropeC = ar.alloc("ropeC", 1024, F32, parts=64)
        ropeS = ar.alloc("ropeS", 1024, F32, parts=64)
        cq = ar.alloc("cq", 4 * 1024, shape=[4, 1024])
        ckv = ar.alloc("ckv", 2 * S, shape=[2, S])
        kpe = ar.alloc("kpe", S)
        Wn = ar.alloc("Wn", 4 * 1024, shape=[4, 1024])
        Wp_ = ar.alloc("Wp", 4 * 512, shape=[4, 512])
        Wps = ar.alloc("Wps", 4 * 512, shape=[4, 512])
        Wk = ar.alloc("Wk", 2 * 1024, shape=[2, 1024])
        Wv = ar.alloc("Wv", 2 * 1024, shape=[2, 1024])
        kn = [ar.alloc("kn%d" % i, S) for i in range(2)]
        vv = [ar.alloc("vv%d" % i, 32 * 128, shape=[32, 128]) for i in range(2)]
        qn = [ar.alloc("qn%d" % i, 1024) for i in range(2)]
        qp = [ar.alloc("qp%d" % i, 1024) for i in range(2)]
        wtm = [ar.alloc("wtm%d" % i, 512) for i in range(4)]
        dacc = [ar.alloc("dacc%d" % i, 512, F32) for i in range(2)]
        k.dma("sp", cq, cqT_d[:, :, tok0:tok0 + 1024])
        k.dma("sp", ckv[:, :, 0:nk], ckvT_d[:, :, 0:nk])
        k.memset("dve", kpe[64:128, :], 0.0)
        for qp_ in qp:
            k.memset("dve", qp_[64:128, :], 0.0)
        k.dma("sp", kpe[0:64, 0:nk], kpeT_d[:, 0:nk])
        k.dma("sp", ropeC, ropeC_d[:, tok0:tok0 + 1024])
        k.dma("sp", ropeS, ropeS_d[:, tok0:tok0 + 1024])
        load_panel(Wn, w_uqn, 0, 4, 0, 1024)
        load_panel(Wp_, w_uqp, 0, 4, 0, 512)
        load_panel(Wps, w_uqps, 0, 4, 0, 512)
        load_panel(Wk, w_ukvk, 0, 2, 0, 1024)
        load_panel(Wv, w_ukvv, 0, 2, 0, 1024)
        qscale = 192.0 ** -0.5

        def mla_head_prep(hd):
            qn_ = qn[hd % 2]
            qp_ = qp[hd % 2]
            kn_ = kn[hd % 2]
            vv_ = vv[hd % 2]
            for hf in range(2):
                cs = slice(hf * 512, (hf + 1) * 512)
                bank = ps()
                for kc in range(4):
                    k.mm(bank, Wn[:, kc, hd * 128:(hd + 1) * 128], cq[:, kc, cs], kc == 0, kc == 3)
                evac(qn_[:, cs], bank, scale=qscale)
                b1 = ps()
                b2 = ps()
                for kc in range(4):
                    k.mm(b1[0:64, :], Wp_[:, kc, hd * 64:(hd + 1) * 64], cq[:, kc, cs], kc == 0, kc == 3)
                for kc in range(4):
                    k.mm(b2[0:64, :], Wps[:, kc, hd * 64:(hd + 1) * 64], cq[:, kc, cs], kc == 0, kc == 3)
                k.tt("dve", tmpf[0][0:64, :], b1[0:64, :], ropeC[:, cs], ALU.mult)
                k.tt("dve", tmpf[1][0:64, :], b2[0:64, :], ropeS[:, cs], ALU.mult)
                k.tt("dve", tmpf[0][0:64, :], tmpf[0][0:64, :], tmpf[1][0:64, :], ALU.add)
                k.ts("dve", qp_[0:64, cs], tmpf[0][0:64, :], qscale, ALU.mult)
            for c in range(nk // 512):
                bank = ps()
                for kc in range(2):
                    k.mm(bank, Wk[:, kc, hd * 128:(hd + 1) * 128], ckv[:, kc, c * 512:(c + 1) * 512], kc == 0, kc == 1)
                evac(kn_[:, c * 512:(c + 1) * 512], bank)
            for b4 in range(nkb_max // 4):
                bank = ps()
                for j in range(4):
                    blk = b4 * 4 + j
                    for kc in range(2):
                        k.mm(bank[:, j * 128:(j + 1) * 128], ckv[:, kc, blk * 128:(blk + 1) * 128],
                             Wv[:, kc, hd * 128:(hd + 1) * 128], kc == 0, kc == 1)
                evac(vv_[:, b4 * 4:(b4 + 1) * 4, :], bank.re("p (j c) -> p j c", j=4))

        units = []
        for hd in range(8):
            for gq in range(2):
                s0, n0, nmax = group_geom(gq)
                for kb in range(nmax):
                    units.append((hd, gq, kb, n0, nmax))

        def mla_s1(u, i):
            hd, gq, kb, n0, nmax = u
            if gq == 0 and kb == 0:
                mla_head_prep(hd)
            j0, mi = kb_geom(n0, kb)
            c0 = j0 * 128
            cs = slice(gq * 512 + c0, gq * 512 + 512)
            A = ps()
            k.mm(A[:, c0:512], kn[hd % 2][:, kb * 128:(kb + 1) * 128], qn[hd % 2][:, cs], True, False)
            k.mm(A[:, c0:512], kpe[:, kb * 128:(kb + 1) * 128], qp[hd % 2][:, cs], False, True)
            w_ = wtm[i % 4]
            k.act(w_[:, c0:512], A[:, c0:512], AF.Exp)
            if mi is not None:
                k.tt("pool", w_[:, c0:c0 + 128], w_[:, c0:c0 + 128], mmla[:, mi * 128:(mi + 1) * 128], ALU.mult)

        def mla_s2(u, i):
            hd, gq, kb, n0, nmax = u
            j0, mi = kb_geom(n0, kb)
            c0 = j0 * 128
            w_ = wtm[i % 4]
            O = banks[6 + gq % 2]
            acc = dacc[gq % 2]
            if kb == 0:
                k.memset("dve", O, 0.0)
                k.memset("dve", acc, 0.0)
            k.mm(O[:, c0:512], vv[hd % 2][:, kb, :], w_[:, c0:512], False, kb == nmax - 1)
            k.tt("dve", acc[:, c0:512], acc[:, c0:512], w_[:, c0:512], ALU.add)
            if kb == nmax - 1:
                Dn = ps()
                k.mm(Dn, ones32, acc, True, True)
                k.recip(tmpf[2], Dn)
                k.tt("dve", of32[8 + hd][:, gq * 512:(gq + 1) * 512], O, tmpf[2], ALU.mult)

        skewed(units, (mla_s1, mla_s2), (0, 2))
        p.barrier()
        if stage == 5:
            for i in range(16):
                k.dma("sp", dbg_of[i], of32[i])
            return finish()

        ar.reset(base_off)
        xns = [ar.alloc("xn%d" % i, D, F32) for i in range(2)]
        xn = xns[0]
        oT_k, oTn = ar.multi("oTn", 16, 1024)
        h2_off = ar.off
        h2_k, hT2 = ar.multi("hT2", 16, 1024)
        wpan = [ar.alloc("wpan%d" % i, 16 * 512) for i in range(2)]
        sqw = [ar.alloc("sqw%d" % i, 512) for i in range(4)]
        for grp in range(2):
            gname = "sb" if grp == 0 else "mla"
            for gq in range(2):
                cs = slice(gq * 512, (gq + 1) * 512)
                SS = banks[6 + gq % 2]
                for hd in range(8):
                    sq = sqw[hd % 4]
                    k.act(sq, of32[grp * 8 + hd][:, cs], AF.Square)
                    k.mm(SS, ones, sq, hd == 0, hd == 7)
                k.act(tmpf[0], SS, AF.Ln, bias=EPS, scale=1.0 / 1024)
                k.act(tmpf[1], tmpf[0], AF.Exp, scale=-0.5)
                for hd in range(8):
                    k.stt(oT_k[grp * 8 + hd][:, cs], of32[grp * 8 + hd][:, cs], g_of(gname, hd), tmpf[1],
                          ALU.mult, ALU.mult)

        xs = []
        for tt_ in range(8):
            xs.append(T(AR[:, tt_ * D:(tt_ + 1) * D], of32[2 * tt_].bufs + of32[2 * tt_ + 1].bufs))
        wi = {"i": 0}

        def next_panel():
            w = wpan[wi["i"] % 2]
            wi["i"] += 1
            return w

        for tt_ in range(8):
            k.dma("sp", xs[tt_], xown[tok0 + tt_ * 128: tok0 + (tt_ + 1) * 128, :])

        def dense_residual(src_k, w_d):
            for cp in range(4):
                w = next_panel().re("p (a b) -> p a b", a=16)
                load_panel(w, w_d, 0, 16, cp * 512, 512)
                for tt_ in range(8):
                    bank = ps(8)
                    for kc in range(16):
                        k.mm(bank, src_k[kc][:, tt_ * 128:(tt_ + 1) * 128], w[:, kc, :], kc == 0, kc == 15)
                    xv = xs[tt_][:, cp * 512:(cp + 1) * 512]
                    k.tt("dve", xv, bank, xv, ALU.add)

        def normT(gname, dst_k):
            def nrm(tt_):
                xn_ = xns[tt_ % 2]
                ss = st1()
                k.act(xn_, xs[tt_], AF.Square, accum=ss)
                r = rstd_of(ss, D)
                k.ts("dve", xn_, xs[tt_], r, ALU.mult)

            nrm(0)
            for tt_ in range(8):
                if tt_ + 1 < 8:
                    nrm(tt_ + 1)
                xn_ = xns[tt_ % 2]
                for k4 in range(4):
                    bank = ps(8)
                    for j in range(4):
                        kc = k4 * 4 + j
                        k.tr(bank[:, j * 128:(j + 1) * 128], xn_[:, kc * 128:(kc + 1) * 128], ident)
                    for j in range(4):
                        kc = k4 * 4 + j
                        evac(dst_k[kc][:, tt_ * 128:(tt_ + 1) * 128], bank[:, j * 128:(j + 1) * 128],
                             scale=g_of(gname, kc), eng=("act" if k4 % 2 else "dve"))

        dense_residual(oT_k, w_out)
        if stage == 6:
            dump_x(xs, tok0)
            return finish()
        normT("mem", h2_k)
        qm_k = oT_k
        for cp in range(4):
            w = next_panel().re("p (a b) -> p a b", a=16)
            load_panel(w, w_mq, 0, 16, cp * 512, 512)
            for c in range(4):
                for hf in range(2):
                    cs = slice(hf * 512, (hf + 1) * 512)
                    bank = ps(8)
                    for kc in range(16):
                        k.mm(bank, w[:, kc, c * 128:(c + 1) * 128], h2_k[kc][:, cs], kc == 0, kc == 15)
                    evac(qm_k[cp * 4 + c][:, cs], bank, scale=512.0 ** -0.5, eng=("act" if c % 2 else "dve"))
        om_k = h2_k
        rcp = tmpf[2]
        for hm in range(4):
            for hf in range(2):
                cs = slice(hf * 512, (hf + 1) * 512)
                ws = []
                for mb in range(2):
                    bank = ps(8)
                    for c4 in range(4):
                        k.mm(bank, kmT[:, hm * 4 + c4, mb * 128:(mb + 1) * 128], qm_k[hm * 4 + c4][:, cs], c4 == 0, c4 == 3)
                    w_ = sqw[(hf * 2 + mb) % 4]
                    k.act(w_, bank, AF.Exp)
                    ws.append(w_)
                Dn = ps(8)
                for mb in range(2):
                    k.mm(Dn, ones, ws[mb], mb == 0, mb == 1)
                k.recip(rcp, Dn)
                for c4 in range(4):
                    bank = ps(8)
                    for mb in range(2):
                        k.mm(bank, vm[:, mb, (hm * 4 + c4) * 128:(hm * 4 + c4 + 1) * 128], ws[mb], mb == 0, mb == 1)
                    k.tt("dve", om_k[hm * 4 + c4][:, cs], bank, rcp, ALU.mult)
        dense_residual(om_k, w_mo)
        if stage == 7:
            dump_x(xs, tok0)
            return finish()
        h3_k = oT_k
        normT("ffn", h3_k)
        for g in range(16):
            w1 = next_panel().re("p (a b) -> p a b", a=16)
            load_panel(w1, w_ff1, 0, 16, g * 512, 512)
            w2 = next_panel().re("p (a b) -> p a b", a=4)
            load_panel(w2, w_ff2, g * 512, 4, 0, D)
            u_k = h2_k[(g % 2) * 4:(g % 2) * 4 + 4]
            for j in range(4):
                for hf in range(2):
                    cs = slice(hf * 512, (hf + 1) * 512)
                    bank = ps(8)
                    for kc in range(16):
                        k.mm(bank, w1[:, kc, j * 128:(j + 1) * 128], h3_k[kc][:, cs], kc == 0, kc == 15)
                    sq = tmpf[(j * 2 + hf) % 2]
                    k.act(sq, bank, AF.Square)
                    k.stt(u_k[j][:, cs], bank, 0.0, sq, ALU.is_gt, ALU.mult)
            for tt_ in range(8):
                for cb in range(4):
                    bank = ps(8)
                    for j in range(4):
                        k.mm(bank, u_k[j][:, tt_ * 128:(tt_ + 1) * 128], w2[:, j, cb * 512:(cb + 1) * 512], j == 0, j == 3)
                    xv = xs[tt_][:, cb * 512:(cb + 1) * 512]
                    k.tt("dve", xv, bank, xv, ALU.add)
        if stage == 8:
            dump_x(xs, tok0)
            return finish()
        gfin = T(AR[:, h2_off + 8 * 512: h2_off + 12 * 512], [b_ for t_ in h2_k[8:12] for b_ in t_.bufs])
        k.dma("sp", gfin, gfin_d)
        for tt_ in range(8):
            ss = st1()
            k.act(xn, xs[tt_], AF.Square, accum=ss)
            r = rstd_of(ss, D)
            k.stt(xn, xs[tt_], r, gfin, ALU.mult, ALU.mult)
            k.dma("sp", out_d[tok0 + tt_ * 128: tok0 + (tt_ + 1) * 128, :], xn)
        p.barrier()

    p.emit(nc, stack)
    stack.close()
    return nc


def own_blocks(h):
    out = []
    for g in range(8):
        out += [4 * g, 4 * g + 3] if h == 0 else [4 * g + 1, 4 * g + 2]
    return out


def _consts(h):
    j = np.arange(128)[:, None]
    s = np.arange(128)[None, :]
    ident = np.eye(128, dtype=np.float32)
    negU = -(j >= s).astype(np.float32)
    negones = -np.ones((128, 128), np.float32)
    ones = np.ones((128, 128), np.float32)
    sk = np.arange(128)[:, None]
    tq = np.arange(128)[None, :]
    diag_sb = (sk < tq).astype(np.float32)
    diag_mla = ((sk // 64) <= (tq // 64)).astype(np.float32)
    full = np.ones((128, 128), np.float32)
    zero = np.zeros((128, 128), np.float32)

    def four(diag):
        if h == 0:
            return [diag, zero, full, diag]
        return [full, diag, diag, zero]

    return np.concatenate([ident, negU, negones, ones, zero] + four(diag_sb) + four(diag_mla), axis=1)


_NC_CACHE = {}


def make_in_maps(x, mem, positions, norm_mix_g, w_in, q_norm_g, w_uq, kv_norm_g, w_ukv, gn_sb_g, gn_mla_g, w_out,
                 norm_mem_g, mem_kv_norm_g, w_mq, w_mkv, w_mo, norm_ffn_g, w_ff1, w_ff2, final_norm_g):
    x = np.asarray(x, np.float32)
    mem = np.asarray(mem, np.float32)
    positions = np.asarray(positions, np.int32)
    f = lambda a: np.ascontiguousarray(np.asarray(a, np.float32))
    w_in0 = f(w_in)[0]
    w_uq0 = f(w_uq)[0].reshape(512, 8, 192)
    w_ukv0 = f(w_ukv)[0].reshape(256, 8, 256)
    kpe = w_in0[:, 3840:3904]
    w_kpesw = np.ascontiguousarray(np.concatenate([kpe[:, 32:], kpe[:, :32]], axis=1))
    w_uqn = np.ascontiguousarray(w_uq0[:, :, :128].reshape(512, 1024))
    w_uqp = np.ascontiguousarray(w_uq0[:, :, 128:].reshape(512, 512))
    w_uqps = np.ascontiguousarray(np.concatenate([w_uq0[:, :, 160:], w_uq0[:, :, 128:160]], axis=2).reshape(512, 512))
    w_ukvk = np.ascontiguousarray(w_ukv0[:, :, :128].reshape(256, 1024))
    w_ukvv = np.ascontiguousarray(w_ukv0[:, :, 128:].reshape(256, 1024))

    def gT(g, n):
        return np.asarray(g, np.float32).reshape(n, 128).T

    gpack = np.zeros((128, 96), np.float32)
    gpack[:, 0:16] = gT(norm_mix_g, 16)
    gpack[:, 16:32] = gT(norm_mem_g, 16)
    gpack[:, 32:48] = gT(norm_ffn_g, 16)
    gpack[:, 48:64] = gT(mem_kv_norm_g, 16)
    gpack[:, 64:68] = gT(q_norm_g, 4)
    gpack[:, 68:70] = gT(kv_norm_g, 2)
    gpack[:, 72:80] = gT(gn_sb_g, 8)
    gpack[:, 80:88] = gT(gn_mla_g, 8)
    gfin = np.ascontiguousarray(np.broadcast_to(np.asarray(final_norm_g, np.float32).reshape(1, D), (128, D)))
    half = 32
    inv_freq = (np.float32(10000.0) ** (-np.arange(half, dtype=np.float32) / np.float32(half))).astype(np.float32)
    rpack = np.zeros((64, 2), np.float32)
    rpack[:, 0] = np.concatenate([inv_freq, inv_freq])
    rpack[:, 1] = np.concatenate([-np.ones(32, np.float32), np.ones(32, np.float32)])
    shared = {
        "w_in": w_in0, "w_kpesw": w_kpesw, "w_uqn": w_uqn, "w_uqp": w_uqp, "w_uqps": w_uqps,
        "w_ukvk": w_ukvk, "w_ukvv": w_ukvv,
        "w_out": f(w_out)[0], "w_mq": f(w_mq)[0], "w_mkv": f(w_mkv)[0], "w_mo": f(w_mo)[0],
        "w_ff1": f(w_ff1)[0], "w_ff2": f(w_ff2)[0],
        "gpack": gpack, "gfin": gfin, "rpack": rpack,
    }
    in_maps = []
    for c in range(8):
        b, h = divmod(c, 2)
        ob = own_blocks(h)
        idx = np.concatenate([np.arange(bk * 128, (bk + 1) * 128) for bk in ob])
        m = dict(shared)
        m.update({
            "xall": np.ascontiguousarray(x[b]),
            "xown": np.ascontiguousarray(x[b][idx]),
            "memx": np.ascontiguousarray(mem[b]),
            "posall": np.ascontiguousarray(positions[b].reshape(1, S)),
            "posown": np.ascontiguousarray(positions[b][idx].reshape(1, NOWN)),
            "cpack": _consts(h),
        })
        in_maps.append(m)
    return in_maps


def kernel(**inputs):
    in_maps = make_in_maps(**inputs)
    if "nc" not in _NC_CACHE:
        _NC_CACHE["nc"] = build_program()
    nc = _NC_CACHE["nc"]
    res = run_bass_kernel_spmd(nc, in_maps, core_ids=list(range(8)))
    out = np.zeros((4, S, D), np.float32)
    for c in range(8):
        b, h = divmod(c, 2)
        ob = own_blocks(h)
        idx = np.concatenate([np.arange(bk * 128, (bk + 1) * 128) for bk in ob])
        out[b][idx] = np.asarray(res.results[c]["out"], np.float32)
    return out
```
